# Optimizing a Trainium2 kernel written in Bass

```python
import math
import jax, jax.numpy as jnp
from jax import lax
import numpy as np

D_MODEL = 1024
BATCH = 4
SEQ = 4096
DEPTH = 2

ATT_HEADS = 8
ATT_HEAD_DIM = 64
ATT_WIDTH = ATT_HEADS * ATT_HEAD_DIM
DILATED_PATTERNS = ((128, 1), (512, 4), (2048, 16))
ATT_BLOCK = 128
ROPE_THETA = 500000.0
ROPE_DIM = ATT_HEAD_DIM // 4

MLSTM_HEADS = 4
MLSTM_HEAD_DIM = 128
MLSTM_WIDTH = MLSTM_HEADS * MLSTM_HEAD_DIM
MLSTM_CHUNK = 64
CONV_WIDTH = 4
FORGET_BIAS = 3.0

D_FF = 4 * D_MODEL
NORM_EPS = 1e-6

COL_AQ = 0
COL_AK = COL_AQ + ATT_WIDTH
COL_AV = COL_AK + ATT_WIDTH
COL_MQ = COL_AV + ATT_WIDTH
COL_MK = COL_MQ + MLSTM_WIDTH
COL_MV = COL_MK + MLSTM_WIDTH
COL_MO = COL_MV + MLSTM_WIDTH
COL_MI = COL_MO + MLSTM_WIDTH
COL_MF = COL_MI + MLSTM_HEADS
COL_GA = COL_MF + MLSTM_HEADS
COL_GB = COL_GA + D_MODEL
D_IN = COL_GB + D_MODEL

kernel_name = "hybrid_dilated_attn_mlstm_gated_block"


def rms_norm(x, w):
    xf = x.astype(jnp.float32)
    y = xf * lax.rsqrt(jnp.mean(xf * xf, axis=-1, keepdims=True) + NORM_EPS)
    return (y * w.astype(jnp.float32)).astype(x.dtype)


def rope_tables(positions):
    inv_freq = ROPE_THETA ** (-jnp.arange(0, ROPE_DIM, 2, dtype=jnp.float32) / ROPE_DIM)
    ang = positions.astype(jnp.float32)[..., None] * inv_freq
    return jnp.cos(ang)[:, :, None, :], jnp.sin(ang)[:, :, None, :]


def apply_partial_rope(t, cos, sin):
    half = ROPE_DIM // 2
    t1 = t[..., :half]
    t2 = t[..., half:ROPE_DIM]
    return jnp.concatenate([t1 * cos - t2 * sin, t2 * cos + t1 * sin, t[..., ROPE_DIM:]], axis=-1)


def dilated_window_attention(q, k, v, window, dilation):
    B, S, H, Dh = q.shape
    span = window // dilation
    L = S // dilation
    nb = -(-L // ATT_BLOCK)
    Lp = nb * ATT_BLOCK
    BD = B * dilation

    def to_sub(t):
        return t.reshape(B, L, dilation, H, Dh).transpose(0, 2, 1, 3, 4).reshape(BD, L, H, Dh)

    qs, ks, vs = to_sub(q), to_sub(k), to_sub(v)
    qb = jnp.pad(qs, ((0, 0), (0, Lp - L), (0, 0), (0, 0))).reshape(BD, nb, ATT_BLOCK, H, Dh)

    def key_blocks(t):
        tp = jnp.pad(t, ((0, 0), (ATT_BLOCK, Lp - L), (0, 0), (0, 0)))
        tp = tp.reshape(BD, nb + 1, ATT_BLOCK, H, Dh)
        return jnp.concatenate([tp[:, :-1], tp[:, 1:]], axis=2)

    kb, vb = key_blocks(ks), key_blocks(vs)
    blk = jnp.arange(nb)[:, None, None]
    qa = jnp.arange(ATT_BLOCK)[None, :, None]
    kc = jnp.arange(2 * ATT_BLOCK)[None, None, :]
    dist = qa + ATT_BLOCK - kc
    key_idx = blk * ATT_BLOCK - ATT_BLOCK + kc
    mask = (dist >= 0) & (dist <= span) & (key_idx >= 0)

    s = jnp.einsum('bnqhd,bnkhd->bnhqk', qb, kb) * (Dh ** -0.5)
    s = jnp.where(mask[None, :, None], s, -jnp.inf)
    m = jnp.max(s, axis=-1, keepdims=True)
    p = jnp.exp(s - m)
    l = jnp.sum(p, axis=-1)
    o = jnp.einsum('bnhqk,bnkhd->bnqhd', p, vb) / jnp.transpose(l, (0, 1, 3, 2))[..., None]
    lse = jnp.transpose(m[..., 0] + jnp.log(l), (0, 1, 3, 2))

    o = o.reshape(BD, Lp, H, Dh)[:, :L].reshape(B, dilation, L, H, Dh)
    o = o.transpose(0, 2, 1, 3, 4).reshape(B, S, H, Dh)
    lse = lse.reshape(BD, Lp, H)[:, :L].reshape(B, dilation, L, H)
    lse = lse.transpose(0, 2, 1, 3).reshape(B, S, H)
    return o, lse


def dilated_mixture_attention(q, k, v):
    outs, lses = [], []
    for window, dilation in DILATED_PATTERNS:
        o, lse = dilated_window_attention(q, k, v, window, dilation)
        outs.append(o)
        lses.append(lse)
    wts = jax.nn.softmax(jnp.stack(lses), axis=0)
    return jnp.einsum('gbsh,gbshd->bshd', wts, jnp.stack(outs))


def causal_short_conv(u, w, b):
    S = u.shape[1]
    up = jnp.pad(u, ((0, 0), (CONV_WIDTH - 1, 0), (0, 0)))
    out = b
    for j in range(CONV_WIDTH):
        out = out + w[j] * up[:, j:j + S]
    return out


def mlstm_chunkwise(q, k, v, ig, fg):
    B, S, H, D = q.shape
    L = MLSTM_CHUNK
    NC = S // L

    def chunks(t):
        return t.reshape(B, NC, L, H, D).transpose(0, 3, 1, 2, 4)

    def gchunks(t):
        return t.reshape(B, NC, L, H).transpose(0, 3, 1, 2)

    q = chunks(q) * (D ** -0.5)
    k = chunks(k)
    v = chunks(v)
    ig = gchunks(ig)
    lf = gchunks(jax.nn.log_sigmoid(fg))
    bcum = jnp.cumsum(lf, axis=-1)
    g = bcum[..., -1]

    a = g[..., None] - bcum + ig
    m_loc = jnp.max(a, axis=-1)
    wk = jnp.exp(a - m_loc[..., None])
    kv_c = jnp.einsum('bhcl,bhclk,bhclv->bhckv', wk, k, v)
    n_c = jnp.einsum('bhcl,bhclk->bhck', wk, k)

    def step(carry, xs):
        C, n, m = carry
        g_c, ml, kv, nn = xs
        m_new = jnp.maximum(g_c + m, ml)
        decay = jnp.exp(g_c + m - m_new)
        inj = jnp.exp(ml - m_new)
        C_new = decay[..., None, None] * C + inj[..., None, None] * kv
        n_new = decay[..., None] * n + inj[..., None] * nn
        return (C_new, n_new, m_new), (C, n, m)

    init = (jnp.zeros((B, H, D, D), jnp.float32), jnp.zeros((B, H, D), jnp.float32),
            jnp.zeros((B, H), jnp.float32))
    xs = (g.transpose(2, 0, 1), m_loc.transpose(2, 0, 1),
          kv_c.transpose(2, 0, 1, 3, 4), n_c.transpose(2, 0, 1, 3))
    _, (C_prev, n_prev, m_prev) = lax.scan(step, init, xs)
    C_prev = C_prev.transpose(1, 2, 0, 3, 4)
    n_prev = n_prev.transpose(1, 2, 0, 3)
    m_prev = m_prev.transpose(1, 2, 0)

    causal = jnp.tril(jnp.ones((L, L), dtype=bool))
    dmat = bcum[..., :, None] - bcum[..., None, :] + ig[..., None, :]
    dmat = jnp.where(causal, dmat, -jnp.inf)
    m_inter = bcum + m_prev[..., None]
    m_t = jnp.maximum(jnp.max(dmat, axis=-1), m_inter)
    p = jnp.exp(dmat - m_t[..., None]) * jnp.einsum('bhcld,bhcsd->bhcls', q, k)
    w_inter = jnp.exp(m_inter - m_t)
    num = (w_inter[..., None] * jnp.einsum('bhcld,bhcdv->bhclv', q, C_prev)
           + jnp.einsum('bhcls,bhcsv->bhclv', p, v))
    den = w_inter * jnp.einsum('bhcld,bhcd->bhcl', q, n_prev) + jnp.sum(p, axis=-1)
    h = num / jnp.maximum(jnp.abs(den), jnp.exp(-m_t))[..., None]
    return h.transpose(0, 2, 3, 1, 4).reshape(B, S, H, D)


def hybrid_layer(x, cos, sin, norm_mix_w, w_in, b_in, conv_w, conv_b, mlstm_norm_w,
                 w_proj_att, w_proj_mlstm, w_out, norm_mlp_w, w_ff1, w_ff2):
    B, S, _ = x.shape
    f32 = jnp.float32
    h = rms_norm(x, norm_mix_w)
    z = h @ w_in + b_in

    aq = z[..., COL_AQ:COL_AK].astype(f32).reshape(B, S, ATT_HEADS, ATT_HEAD_DIM)
    ak = z[..., COL_AK:COL_AV].astype(f32).reshape(B, S, ATT_HEADS, ATT_HEAD_DIM)
    av = z[..., COL_AV:COL_MQ].astype(f32).reshape(B, S, ATT_HEADS, ATT_HEAD_DIM)
    aq = apply_partial_rope(aq, cos, sin)
    ak = apply_partial_rope(ak, cos, sin)
    ya = dilated_mixture_attention(aq, ak, av).reshape(B, S, ATT_WIDTH).astype(x.dtype)

    qk = jax.nn.silu(causal_short_conv(z[..., COL_MQ:COL_MV].astype(f32),
                                       conv_w.astype(f32), conv_b.astype(f32)))
    mq = qk[..., :MLSTM_WIDTH].reshape(B, S, MLSTM_HEADS, MLSTM_HEAD_DIM)
    mk = qk[..., MLSTM_WIDTH:].reshape(B, S, MLSTM_HEADS, MLSTM_HEAD_DIM)
    mv = z[..., COL_MV:COL_MO].astype(f32).reshape(B, S, MLSTM_HEADS, MLSTM_HEAD_DIM)
    mi = z[..., COL_MI:COL_MF].astype(f32)
    mf = z[..., COL_MF:COL_GA].astype(f32)
    hm = mlstm_chunkwise(mq, mk, mv, mi, mf)
    hm = hm * lax.rsqrt(jnp.mean(hm * hm, axis=-1, keepdims=True) + NORM_EPS)
    hm = hm.reshape(B, S, MLSTM_WIDTH) * mlstm_norm_w.astype(f32)
    yb = (jax.nn.sigmoid(z[..., COL_MO:COL_MI].astype(f32)) * hm).astype(x.dtype)

    ga = jax.nn.sigmoid(z[..., COL_GA:COL_GB])
    gb = jax.nn.sigmoid(z[..., COL_GB:D_IN])
    mixed = ga * (ya @ w_proj_att) + gb * (yb @ w_proj_mlstm)
    x = x + mixed @ w_out

    h2 = rms_norm(x, norm_mlp_w)
    x = x + jnp.square(jax.nn.relu(h2 @ w_ff1)) @ w_ff2
    return x


def setup_inputs(seed: int = 0) -> dict:
    key = jax.random.key(seed)
    ks = jax.random.split(key, 16)
    nrm = jax.random.normal
    x = nrm(ks[0], (BATCH, SEQ, D_MODEL), jnp.float32)
    offsets = jax.random.randint(ks[1], (BATCH, 1), 0, 1024, dtype=jnp.int32)
    positions = offsets + jnp.arange(SEQ, dtype=jnp.int32)[None, :]
    norm_mix_w = 1.0 + 0.05 * nrm(ks[2], (DEPTH, D_MODEL), jnp.float32)
    w_in = nrm(ks[3], (DEPTH, D_MODEL, D_IN), jnp.float32) * D_MODEL ** -0.5
    b_in = 0.02 * nrm(ks[4], (DEPTH, D_IN), jnp.float32)
    b_in = b_in.at[:, COL_MF:COL_GA].add(FORGET_BIAS)
    conv_w = nrm(ks[5], (DEPTH, CONV_WIDTH, 2 * MLSTM_WIDTH), jnp.float32) * CONV_WIDTH ** -0.5
    conv_b = 0.02 * nrm(ks[6], (DEPTH, 2 * MLSTM_WIDTH), jnp.float32)
    mlstm_norm_w = 1.0 + 0.05 * nrm(ks[7], (DEPTH, MLSTM_WIDTH), jnp.float32)
    w_proj_att = nrm(ks[8], (DEPTH, ATT_WIDTH, D_MODEL), jnp.float32) * ATT_WIDTH ** -0.5
    w_proj_mlstm = nrm(ks[9], (DEPTH, MLSTM_WIDTH, D_MODEL), jnp.float32) * MLSTM_WIDTH ** -0.5
    w_out = nrm(ks[10], (DEPTH, D_MODEL, D_MODEL), jnp.float32) * D_MODEL ** -0.5
    norm_mlp_w = 1.0 + 0.05 * nrm(ks[11], (DEPTH, D_MODEL), jnp.float32)
    w_ff1 = nrm(ks[12], (DEPTH, D_MODEL, D_FF), jnp.float32) * D_MODEL ** -0.5
    w_ff2 = nrm(ks[13], (DEPTH, D_FF, D_MODEL), jnp.float32) * (0.5 * D_FF ** -0.5)
    final_norm_w = 1.0 + 0.05 * nrm(ks[14], (D_MODEL,), jnp.float32)
    return {"x": x, "positions": positions, "norm_mix_w": norm_mix_w, "w_in": w_in,
            "b_in": b_in, "conv_w": conv_w, "conv_b": conv_b, "mlstm_norm_w": mlstm_norm_w,
            "w_proj_att": w_proj_att, "w_proj_mlstm": w_proj_mlstm, "w_out": w_out,
            "norm_mlp_w": norm_mlp_w, "w_ff1": w_ff1, "w_ff2": w_ff2,
            "final_norm_w": final_norm_w}


def reference(x, positions, norm_mix_w, w_in, b_in, conv_w, conv_b, mlstm_norm_w,
              w_proj_att, w_proj_mlstm, w_out, norm_mlp_w, w_ff1, w_ff2, final_norm_w):
    cos, sin = rope_tables(positions)
    for l in range(DEPTH):
        x = hybrid_layer(x, cos, sin, norm_mix_w[l], w_in[l], b_in[l], conv_w[l], conv_b[l],
                         mlstm_norm_w[l], w_proj_att[l], w_proj_mlstm[l], w_out[l],
                         norm_mlp_w[l], w_ff1[l], w_ff2[l])
    return rms_norm(x, final_norm_w)
```

```python
import math
from contextlib import ExitStack
import numpy as np
import concourse.bass as bass
import concourse.mybir as mybir

F32 = mybir.dt.float32
BF16 = mybir.dt.bfloat16
AF = mybir.ActivationFunctionType
ALU = mybir.AluOpType
AX = mybir.AxisListType

SEM_CAP = 12000
ENGS = ("pe", "act", "dve", "pool", "sp")


class Buf:
    __slots__ = ("name", "last_w", "readers", "dma_total", "dma_sem")

    def __init__(self, name):
        self.name = name
        self.last_w = None
        self.readers = {}
        self.dma_total = 0
        self.dma_sem = None


class Prog:
    def __init__(self, nc):
        self.nc = nc
        self.ops = {e: [] for e in ENGS}
        self.count = {e: 0 for e in ENGS}
        self.seen = {e: {} for e in ENGS}
        self.dma_bufs = []
        self.final_waits = []

    def _deps(self, eng, reads, writes, is_dma):
        deps = {}

        def add(ev, raw):
            if ev is None:
                return
            k, v = ev
            if k[0] == "dma":
                v = k[1].dma_total
            elif k[1] == eng and not is_dma:
                if eng == "pe" or not raw:
                    return
            if deps.get(k, 0) < v:
                deps[k] = v

        for b in reads:
            add(b.last_w, True)
        for b in writes:
            add(b.last_w, False)
            for k, v in b.readers.items():
                add((k, v), False)
        out = []
        seen = self.seen[eng]
        for k, v in deps.items():
            if seen.get(k, 0) < v:
                seen[k] = v
                out.append((k, v))
        return out

    def _commit(self, ev, reads, writes):
        for b in writes:
            b.last_w = ev
            b.readers = {}
        for b in reads:
            k, v = ev
            if b.readers.get(k, 0) < v:
                b.readers[k] = v

    def op(self, eng, fn, reads=(), writes=()):
        waits = self._deps(eng, reads, writes, False)
        self.count[eng] += 1
        ev = (("eng", eng), self.count[eng])
        self.ops[eng].append(("op", fn, waits, self.count[eng]))
        self._commit(ev, reads, writes)
        return ev

    def dma(self, eng, out, in_, reads=(), writes=(), tag=None):
        waits = self._deps(eng, reads, writes, True)
        tb = tag or (writes[0] if writes else reads[0])
        if tb.dma_sem is None:
            tb.dma_sem = "pending"
            self.dma_bufs.append(tb)
        tb.dma_total += 16
        ev = (("dma", tb), tb.dma_total)
        self.ops[eng].append(("dma", (out, in_), waits, tb))
        self._commit(ev, reads, writes)
        return ev

    def wait_all(self, eng, bufs):
        deps = {}
        for b in bufs:
            for ev in [b.last_w] + list(b.readers.items()):
                if ev is None:
                    continue
                k, v = ev
                if k[0] == "dma":
                    v = k[1].dma_total
                if deps.get(k, 0) < v:
                    deps[k] = v
        self.ops[eng].append(("wait", None, list(deps.items()), None))

    def emit(self, stack):
        nc = self.nc
        nsem = {e: (self.count[e] + SEM_CAP - 1) // SEM_CAP for e in ENGS}
        esems = {}
        for e in ENGS:
            esems[e] = [stack.enter_context(nc.semaphore(f"s_{e}_{i}")) for i in range(nsem[e])]
        for i, b in enumerate(self.dma_bufs):
            b.dma_sem = stack.enter_context(nc.semaphore(f"d_{i}"))
        block = stack.enter_context(nc.Block())

        def resolve(k, v):
            if k[0] == "dma":
                return [(k[1].dma_sem, v)]
            e = k[1]
            idx = (v - 1) // SEM_CAP
            return [(esems[e][idx], v - idx * SEM_CAP)]

        def run(ename):
            def body(eng):
                for kind, payload, waits, extra in self.ops[ename]:
                    for k, v in waits:
                        for s, val in resolve(k, v):
                            eng.wait_ge(s, val)
                    if kind == "op":
                        ins = payload(eng)
                        idx = (extra - 1) // SEM_CAP
                        ins.then_inc(esems[ename][idx], 1)
                    elif kind == "dma":
                        o, i = payload
                        eng.dma_start(out=o, in_=i).then_inc(extra.dma_sem, 16)
            return body

        block.tensor(run("pe"))
        block.scalar(run("act"))
        block.vector(run("dve"))
        block.gpsimd(run("pool"))
        block.sync(run("sp"))


def ss(start, n, step=1):
    return slice(start, start + (n - 1) * step + 1, step)


_DSZ = {F32: 4, BF16: 2, mybir.dt.int32: 4}


class Arena:
    LO = 20608
    HI = 229344

    def __init__(self, nc):
        self.nc = nc
        self.off = self.LO
        self.n = 0

    def alloc(self, name, shape, dtype):
        nb = int(np.prod(shape[1:])) * _DSZ[dtype]
        nb = (nb + 63) // 64 * 64
        assert self.off + nb <= self.HI, f"arena overflow at {name}: {self.off + nb - self.HI} over"
        self.n += 1
        t = self.nc.alloc_sbuf_tensor_at(f"{name}_{self.n}", shape, dtype, offset=self.off)
        self.off += nb
        return t

    def mark(self):
        return self.off

    def release(self, m):
        self.off = m


def barrier(P):
    for eng in ENGS:
        waits = []
        seen = P.seen[eng]
        for e2 in ENGS:
            c = P.count[e2]
            if c > 0 and e2 != eng or (c > 0 and eng in ("act", "pool", "sp")):
                k = ("eng", e2)
                if seen.get(k, 0) < c:
                    seen[k] = c
                    waits.append((k, c))
        for b in P.dma_bufs:
            k = ("dma", b)
            if seen.get(k, 0) < b.dma_total:
                seen[k] = b.dma_total
                waits.append((k, b.dma_total))
        if waits:
            P.ops[eng].append(("wait", None, waits, None))


D = 1024
NT = 2048
DFF = 4096
EPS = 1e-6


class Stager:
    def __init__(self, P, nc, st, n=3, width=1024):
        self.P = P
        self.tiles = [st.enter_context(nc.sbuf_tensor(f"stg{i}", [128, width], F32)) for i in range(n)]
        self.bufs = [Buf(f"stg{i}") for i in range(n)]
        self.i = 0
        self.ce = 0
        self.width = width

    def load(self, src, dst, dstbuf, engs=("dve", "pool"), q="sp"):
        n = src.shape[-1]
        if n > self.width:
            for c0 in range(0, n, self.width):
                self.load(src[:, c0:c0 + self.width], dst[:, c0:c0 + self.width], dstbuf, engs, q)
            return
        t = self.tiles[self.i]
        b = self.bufs[self.i]
        self.i = (self.i + 1) % len(self.tiles)
        self.P.dma(q, t[:, 0:n], src, writes=[b])
        ce = engs[self.ce % len(engs)]
        self.ce += 1
        if ce == "act":
            self.P.op("act", lambda e: e.copy(out=dst, in_=t[:, 0:n]), reads=[b], writes=[dstbuf])
        else:
            self.P.op(ce, lambda e: e.tensor_copy(out=dst, in_=t[:, 0:n]), reads=[b], writes=[dstbuf])


def rms_tile(P, xt, xbuf, n, ones_b, Bones, sq, Bsq, ps, Bps, std, Bstd, rstd, Brstd, eps_t, sq_eng="pool"):
    for k in range(8):
        P.op(sq_eng, lambda e, k=k: e.tensor_tensor(out=sq[:, k % 2, 0:n], in0=xt[:, k, 0:n], in1=xt[:, k, 0:n], op=ALU.mult),
             reads=[xbuf], writes=[Bsq[k % 2]])
        P.op("pe", lambda e, k=k: e.matmul(ps[:, 0:n], ones_b[:], sq[:, k % 2, 0:n], start=(k == 0), stop=(k == 7)),
             reads=[Bones, Bsq[k % 2]], writes=[Bps])
    P.op("act", lambda e: e.activation(out=rstd[:, 0:n], in_=ps[:, 0:n], func=AF.Sqrt, bias=eps_t[:, 0:1], scale=1.0 / D),
         reads=[Bps], writes=[Brstd])
    P.op("dve", lambda e: e.reciprocal(out=rstd[:, 0:n], in_=rstd[:, 0:n]), reads=[Brstd], writes=[Brstd])


def build_dense(last):
    nc = bass.Bass("TRN2", target_bir_lowering=False)
    dt = lambda name, shape, dtype, kind="ExternalInput": nc.dram_tensor(name, shape, dtype, kind=kind).ap()
    xT = dt("xT", [D, NT], F32)
    yaT = dt("yaT", [512, NT], BF16)
    ybT = dt("ybT", [512, NT], BF16)
    wg = dt("wg", [D, 2048], F32)
    bg = dt("bg", [128, 16], F32)
    wmix = dt("wmix", [128, 8], F32)
    wmlp = dt("wmlp", [128, 8], F32)
    wfin = dt("wfin", [128, 8], F32)
    pa = dt("pa", [512, D], F32)
    pb = dt("pb", [512, D], F32)
    wout = dt("wout", [D, D], F32)
    w1 = dt("w1", [D, DFF], F32)
    w2 = dt("w2", [DFF, D], F32)
    xoT = dt("xoT", [D, NT], F32, kind="ExternalOutput")
    xmid = dt("xmid", [D, NT], F32, kind="Internal")

    xT_v = xT.rearrange("(k p) t -> p k t", p=128)
    xo_v = xoT.rearrange("(k p) t -> p k t", p=128)
    xm_v = xmid.rearrange("(k p) t -> p k t", p=128)
    ya_v = yaT.rearrange("(k p) t -> p k t", p=128)
    yb_v = ybT.rearrange("(k p) t -> p k t", p=128)
    wg_v = wg.rearrange("(k p) n -> p k n", p=128)
    pa_v = pa.rearrange("(k p) n -> p k n", p=128)
    pb_v = pb.rearrange("(k p) n -> p k n", p=128)
    wout_v = wout.rearrange("(k p) n -> p k n", p=128)
    w1_v = w1.rearrange("(k p) n -> p k n", p=128)
    w2_v = w2.rearrange("(k p) n -> p k n", p=128)

    with ExitStack() as st:
        sb = lambda n, s, d: st.enter_context(nc.sbuf_tensor(n, s, d))
        P = Prog(nc)
        regA = sb("regA", [128, 8, 4096], BF16)
        w2b = sb("w2b", [128, 32, 1024], BF16)
        BregA = [Buf(f"regA{i}") for i in range(8)]
        Bw2 = [Buf(f"w2_{i}") for i in range(16)]
        regA_f = regA[:].rearrange("p k n -> p (k n)")
        wgb = regA_f[:, 0:16384].rearrange("p (k n) -> p k n", k=8)
        pab = regA_f[:, 16384:20480].rearrange("p (k n) -> p k n", k=4)
        pbb = regA_f[:, 20480:24576].rearrange("p (k n) -> p k n", k=4)
        woutb = regA_f[:, 24576:32768].rearrange("p (k n) -> p k n", k=8)
        Bwg, Bpa, Bpb, Bwout = Buf("wg"), Buf("pa"), Buf("pb"), Buf("wout")
        ones_b = sb("ones_b", [128, 128], BF16); Bones = Buf("ones")
        eps_t = sb("eps_t", [128, 1], F32)
        bg_t = sb("bg_t", [128, 16], F32); wmix_t = sb("wmix_t", [128, 8], F32)
        wmlp_t = sb("wmlp_t", [128, 8], F32); wfin_t = sb("wfin_t", [128, 8], F32)
        Bsm = Buf("small")
        P.op("pool", lambda e: e.memset(ones_b[:], 1.0), writes=[Bones])
        P.op("pool", lambda e: e.memset(eps_t[:], EPS), writes=[Bsm])
        for t_, s_ in ((bg_t, bg), (wmix_t, wmix), (wmlp_t, wmlp), (wfin_t, wfin)):
            P.dma("sp", t_[:], s_[:, :], writes=[Bsm])
        xt = [sb("xt0", [128, 8, 512], F32), sb("xt1", [128, 8, 256], F32)]; Bxt = [Buf(f"xt{i}") for i in range(2)]
        sq = sb("sq", [128, 2, 512], BF16); Bsq = [Buf("sq0"), Buf("sq1")]
        std = None; Bstd = Buf("std")
        rstd = sb("rstd", [128, 512], F32); Brstd = Buf("rstd")
        hT = sb("hT", [128, 8, 512], BF16); BhT = Buf("hT")
        yats = [sb("yat0", [128, 4, 512], BF16)] * 2; Byats = [Buf("yat0")] * 2
        ybts = [sb("ybt0", [128, 4, 512], BF16)] * 2; Bybts = [Buf("ybt0")] * 2
        Bxm = [Buf(f"xm{i}") for i in range(NT // 512)]
        gat = sb("gat", [128, 4, 512], BF16); Bgat = [Buf(f"gat{i}") for i in range(4)]
        t1 = [sb(f"t1_{i}", [128, 512], BF16) for i in range(2)]; Bt1 = [Buf(f"t1_{i}") for i in range(2)]
        t2 = [sb(f"t2_{i}", [128, 512], BF16) for i in range(2)]; Bt2 = [Buf(f"t2_{i}") for i in range(2)]
        mixed = sb("mixed", [128, 8, 512], BF16); Bmixed = [Buf(f"mx{i}") for i in range(8)]
        rl = [sb(f"rl{i}", [128, 256], BF16) for i in range(3)]; Brl = [Buf(f"rl{i}") for i in range(3)]
        hq = [sb(f"hq{i}", [128, 256], BF16) for i in range(3)]; Bhq = [Buf(f"hq{i}") for i in range(3)]
        PS = [st.enter_context(nc.psum_tensor(f"ps{i}", [128, 512], F32)) for i in range(8)]
        BPS = [Buf(f"ps{i}") for i in range(8)]
        stg = Stager(P, nc, st)

        for k in range(8):
            stg.load(wg_v[:, k, :], wgb[:, k, :], Bwg)
        for k in range(4):
            stg.load(pa_v[:, k, :], pab[:, k, :], Bpa)
            stg.load(pb_v[:, k, :], pbb[:, k, :], Bpb)
        for k in range(8):
            stg.load(wout_v[:, k, :], woutb[:, k, :], Bwout)

        TT = 512
        psr = 0
        def nextps():
            nonlocal psr
            i = 1 + psr % 5
            psr += 1
            return i
        w2_loaded = 0
        def d1_loads(ti):
            t0 = ti * 512
            xi = ti % 2
            P.dma("sp", xt[0][:], xT_v[:, :, t0:t0 + 512], writes=[Bxt[0]])
            P.dma("sp", yats[xi][:], ya_v[:, :, t0:t0 + 512], writes=[Byats[xi]])
            P.dma("sp", ybts[xi][:], yb_v[:, :, t0:t0 + 512], writes=[Bybts[xi]])
        d1_loads(0)
        for ti in range(NT // TT):
            t0 = ti * TT
            xi = ti % 2
            X, BX = xt[0], Bxt[0]
            yat, Byat, ybt, Bybt = yats[xi], Byats[xi], ybts[xi], Bybts[xi]
            rms_tile(P, X, BX, TT, ones_b, Bones, sq, Bsq, PS[0], BPS[0], std, Bstd, rstd, Brstd, eps_t)
            for k in range(8):
                P.op("dve", lambda e, k=k, X=X: e.scalar_tensor_tensor(out=hT[:, k, :], in0=X[:, k, :], scalar=wmix_t[:, k:k + 1],
                                                                      in1=rstd[:, :], op0=ALU.mult, op1=ALU.mult),
                     reads=[BX, Brstd, Bsm], writes=[BhT])
            def gate(mm, slot):
                pi = nextps()
                for k in range(8):
                    P.op("pe", lambda e, k=k, pi=pi: e.matmul(PS[pi][:, :], wgb[:, k, mm * 128:(mm + 1) * 128], hT[:, k, :],
                                                             start=(k == 0), stop=(k == 7)),
                         reads=[Bwg, BhT], writes=[BPS[pi]])
                P.op("act", lambda e, pi=pi: e.activation(out=gat[:, slot, :], in_=PS[pi][:, :], func=AF.Sigmoid,
                                                         bias=bg_t[:, mm:mm + 1], scale=1.0),
                     reads=[BPS[pi], Bsm], writes=[Bgat[slot]])
            for m in range(8):
                j = m % 2
                gate(m, j)
                gate(8 + m, 2 + j)
                pi = nextps()
                for k in range(4):
                    P.op("pe", lambda e, k=k, m=m, pi=pi: e.matmul(PS[pi][:, :], pab[:, k, m * 128:(m + 1) * 128], yat[:, k, :],
                                                                  start=(k == 0), stop=(k == 3)),
                         reads=[Bpa, Byat], writes=[BPS[pi]])
                P.op("dve", lambda e, m=m, pi=pi, j=j: e.tensor_tensor(out=t1[j][:, :], in0=PS[pi][:, :], in1=gat[:, j, :], op=ALU.mult),
                     reads=[BPS[pi], Bgat[j]], writes=[Bt1[j]])
                pi2 = nextps()
                for k in range(4):
                    P.op("pe", lambda e, k=k, m=m, pi=pi2: e.matmul(PS[pi][:, :], pbb[:, k, m * 128:(m + 1) * 128], ybt[:, k, :],
                                                                   start=(k == 0), stop=(k == 3)),
                         reads=[Bpb, Bybt], writes=[BPS[pi2]])
                P.op("dve", lambda e, m=m, pi=pi2, j=j: e.tensor_tensor(out=t2[j][:, :], in0=PS[pi][:, :], in1=gat[:, 2 + j, :], op=ALU.mult),
                     reads=[BPS[pi2], Bgat[2 + j]], writes=[Bt2[j]])
                P.op("pool", lambda e, m=m, j=j: e.tensor_tensor(out=mixed[:, m, :], in0=t1[j][:, :], in1=t2[j][:, :], op=ALU.add),
                     reads=[Bt1[j], Bt2[j]], writes=[Bmixed[m]])
            for m in range(8):
                pi = nextps()
                for k in range(8):
                    P.op("pe", lambda e, k=k, m=m, pi=pi: e.matmul(PS[pi][:, :], woutb[:, k, m * 128:(m + 1) * 128], mixed[:, k, :],
                                                                  start=(k == 0), stop=(k == 7)),
                         reads=[Bwout, Bmixed[k]], writes=[BPS[pi]])
                P.op("dve", lambda e, m=m, pi=pi, X=X: e.tensor_tensor(out=X[:, m, :], in0=PS[pi][:, :], in1=X[:, m, :], op=ALU.add),
                     reads=[BPS[pi], BX], writes=[BX])
            P.dma("sp", xm_v[:, :, t0:t0 + TT], X[:], reads=[BX], writes=[Bxm[ti]], tag=BX)
            if ti + 1 < NT // TT:
                d1_loads(ti + 1)
            for _ in range(4):
                j = w2_loaded
                stg.load(w2_v[:, 2 * j, :], w2b[:, 2 * j, :], Bw2[j], engs=("pool",))
                stg.load(w2_v[:, 2 * j + 1, :], w2b[:, 2 * j + 1, :], Bw2[j], engs=("pool",))
                w2_loaded += 1

        D1W = [Bwg, Bpa, Bpb, Bwout]
        Bw1 = [Buf(f"w1_{k}") for k in range(8)]
        for k in range(8):
            for q4 in range(4):
                src = w1_v[:, k, q4 * 1024:(q4 + 1) * 1024]
                dst = regA[:, k, q4 * 1024:(q4 + 1) * 1024]
                t = stg.tiles[stg.i]; b = stg.bufs[stg.i]; stg.i = (stg.i + 1) % len(stg.tiles)
                P.dma("sp", t[:, 0:1024], src, writes=[b])
                ce = ("dve", "pool", "act")[(4 * k + q4) % 3]
                if ce == "act":
                    P.op("act", lambda e, dst=dst, t=t: e.copy(out=dst, in_=t[:, 0:1024]), reads=[b], writes=[Bw1[k]] + D1W)
                else:
                    P.op(ce, lambda e, dst=dst, t=t: e.tensor_copy(out=dst, in_=t[:, 0:1024]), reads=[b], writes=[Bw1[k]] + D1W)

        TT = 256
        h2 = hT
        hid_rot = 0
        def d2_loads(ti):
            t0 = ti * 256
            xi = ti % 2
            P.dma("sp", xt[xi][:, :, 0:256], xm_v[:, :, t0:t0 + 256], reads=[Bxm[t0 // 512]], writes=[Bxt[xi]], tag=Bxt[xi])
        d2_loads(0)
        for ti in range(NT // TT):
            t0 = ti * TT
            xi = ti % 2
            X, BX = xt[xi], Bxt[xi]
            rms_tile(P, X, BX, TT, ones_b, Bones, sq, Bsq, PS[0], BPS[0], std, Bstd, rstd, Brstd, eps_t)
            for k in range(8):
                P.op("dve", lambda e, k=k, X=X: e.scalar_tensor_tensor(out=h2[:, k, 0:TT], in0=X[:, k, 0:TT], scalar=wmlp_t[:, k:k + 1],
                                                                      in1=rstd[:, 0:TT], op0=ALU.mult, op1=ALU.mult),
                     reads=[BX, Brstd, Bsm], writes=[BhT])
            for hc in range(32):
                pi = 1 + hid_rot % 3
                r = hid_rot % 3
                hid_rot += 1
                for k in range(8):
                    P.op("pe", lambda e, k=k, hc=hc, pi=pi: e.matmul(PS[pi][:, 0:TT], regA[:, k, hc * 128:(hc + 1) * 128], h2[:, k, 0:TT],
                                                                    start=(k == 0), stop=(k == 7)),
                         reads=[Bw1[k], BhT], writes=[BPS[pi]])
                P.op("act", lambda e, pi=pi, r=r: e.activation(out=rl[r][:, :], in_=PS[pi][:, 0:TT], func=AF.Relu),
                     reads=[BPS[pi]], writes=[Brl[r]])
                P.op("dve", lambda e, r=r: e.tensor_tensor(out=hq[r][:, :], in0=rl[r][:, :], in1=rl[r][:, :], op=ALU.mult),
                     reads=[Brl[r]], writes=[Bhq[r]])
                for m in range(8):
                    bi = 4 + m // 2
                    c0 = (m % 2) * 256
                    P.op("pe", lambda e, m=m, hc=hc, bi=bi, c0=c0, r=r: e.matmul(PS[bi][:, c0:c0 + TT], w2b[:, hc, m * 128:(m + 1) * 128], hq[r][:, :],
                                                                               start=(hc == 0 and m % 2 == 0), stop=(hc == 31), skip_group_check=True),
                         reads=[Bw2[hc // 2], Bhq[r]], writes=[BPS[bi]])
            for m in range(8):
                bi = 4 + m // 2
                c0 = (m % 2) * 256
                P.op("dve", lambda e, m=m, bi=bi, c0=c0, X=X: e.tensor_tensor(out=X[:, m, 0:TT], in0=PS[bi][:, c0:c0 + TT], in1=X[:, m, 0:TT], op=ALU.add),
                     reads=[BPS[bi], BX], writes=[BX])
            if last:
                rms_tile(P, X, BX, TT, ones_b, Bones, sq, Bsq, PS[0], BPS[0], std, Bstd, rstd, Brstd, eps_t)
                for k in range(8):
                    P.op("dve", lambda e, k=k, X=X: e.scalar_tensor_tensor(out=X[:, k, 0:TT], in0=X[:, k, 0:TT], scalar=wfin_t[:, k:k + 1],
                                                                          in1=rstd[:, 0:TT], op0=ALU.mult, op1=ALU.mult),
                         reads=[BX, Brstd, Bsm], writes=[BX])
            if ti + 1 < NT // TT:
                d2_loads(ti + 1)
            P.dma("sp", xo_v[:, :, t0:t0 + TT], X[:, :, 0:TT], reads=[BX], tag=BX)
        P.wait_all("sp", Bxt)
        P.emit(st)
    return nc


D = 1024
S = 4096
EPS = 1e-6
I32 = mybir.dt.int32
PATTERNS = (1, 4, 16)


def rope_tables(P, A, pos_ap, invf_t, sgn_t, Bc, Ctab, Stab, BC, BS, n=S):
    H = n // 2
    posi = A.alloc("posi", [128, H], I32); Bpi = Buf("posi")
    ang = A.alloc("ang", [128, H], F32); Bang = Buf("ang")
    kf = A.alloc("kf", [128, H], F32); Bkf = Buf("kf")
    ki = posi
    TWO_PI = 2 * math.pi
    for hf in range(2):
        cs = slice(hf * H, (hf + 1) * H)
        P.dma("sp", posi[:], pos_ap[:, cs].broadcast_to([128, H]), writes=[Bpi])
        P.op("dve", lambda e: e.tensor_copy(out=ang[:], in_=posi[:]), reads=[Bpi], writes=[Bang])
        P.op("dve", lambda e: e.tensor_scalar(out=ang[:], in0=ang[:], scalar1=invf_t[:, 0:1], scalar2=None, op0=ALU.mult),
             reads=[Bang, Bc], writes=[Bang])

        def wrap(dst, Bdst, add):
            P.op("dve", lambda e: e.tensor_scalar(out=kf[:], in0=ang[:], scalar1=add, scalar2=1.0 / TWO_PI, op0=ALU.add, op1=ALU.mult),
                 reads=[Bang], writes=[Bkf])
            P.op("dve", lambda e: e.tensor_copy(out=ki[:], in_=kf[:]), reads=[Bkf], writes=[Bpi])
            P.op("dve", lambda e: e.tensor_copy(out=kf[:], in_=ki[:]), reads=[Bpi], writes=[Bkf])
            P.op("dve", lambda e: e.scalar_tensor_tensor(out=dst, in0=kf[:], scalar=-TWO_PI, in1=ang[:], op0=ALU.mult, op1=ALU.add),
                 reads=[Bkf, Bang], writes=[Bdst])
            if add != 0.0:
                P.op("dve", lambda e: e.tensor_scalar(out=dst, in0=dst, scalar1=add, scalar2=None, op0=ALU.add),
                     reads=[Bdst], writes=[Bdst])
            P.op("dve", lambda e: e.tensor_scalar(out=kf[:], in0=dst, scalar1=math.pi, scalar2=None, op0=ALU.is_gt),
                 reads=[Bdst], writes=[Bkf])
            P.op("dve", lambda e: e.scalar_tensor_tensor(out=dst, in0=kf[:], scalar=-TWO_PI, in1=dst, op0=ALU.mult, op1=ALU.add),
                 reads=[Bkf, Bdst], writes=[Bdst])
            P.op("act", lambda e: e.activation(out=dst, in_=dst, func=AF.Sin), reads=[Bdst], writes=[Bdst])

        wrap(Stab[:, cs], BS, 0.0)
        P.op("dve", lambda e, cs=cs: e.tensor_scalar(out=Stab[:, cs], in0=Stab[:, cs], scalar1=sgn_t[:, 0:1], scalar2=None, op0=ALU.mult),
             reads=[BS, Bc], writes=[BS])
        wrap(Ctab[:, cs], BC, math.pi / 2)


def build_attn():
    nc = bass.Bass("TRN2", target_bir_lowering=False)
    dt = lambda name, shape, dtype, kind="ExternalInput": nc.dram_tensor(name, shape, dtype, kind=kind).ap()
    xT = dt("xT", [D, S], F32)
    pos = dt("pos", [1, S], I32)
    wmix = dt("wmix", [128, 8], F32)
    wq = dt("wq", [D, 256], F32); wqr = dt("wqr", [D, 256], F32)
    wk = dt("wk", [D, 256], F32); wkr = dt("wkr", [D, 256], F32)
    wv = dt("wv", [D, 256], F32)
    bqkv = dt("bqkv", [128, 10], F32)
    cst = dt("cst", [128, 2], F32)
    mask = dt("mask", [128, 256], BF16)
    ident = dt("ident", [128, 128], BF16)
    yaT = dt("yaT", [256, S], BF16, kind="ExternalOutput")

    xT_v = xT.rearrange("(k p) t -> p k t", p=128)
    ya_v = yaT.rearrange("(c p) t -> p c t", p=128)

    with ExitStack() as st:
        P = Prog(nc)
        A = Arena(nc)
        QT = A.alloc("QT", [128, 2, S], BF16); KT = A.alloc("KT", [128, 2, S], BF16); VT = A.alloc("VT", [128, 2, S], BF16)
        BQ = [[Buf(f"Q{c}_{t}") for t in range(8)] for c in range(2)]
        BK = [[Buf(f"K{c}_{t}") for t in range(8)] for c in range(2)]
        BV = [[Buf(f"V{c}_{t}") for t in range(8)] for c in range(2)]
        mask_t = A.alloc("mask", [128, 256], BF16); ident_t = A.alloc("ident", [128, 128], BF16)
        ones_b = A.alloc("ones", [128, 128], BF16)
        Bc = Buf("consts")
        P.dma("sp", mask_t[:], mask[:, :], writes=[Bc])
        P.dma("sp", ident_t[:], ident[:, :], writes=[Bc])
        P.op("pool", lambda e: e.memset(ones_b[:], 1.0), writes=[Bc])
        PS = [st.enter_context(nc.psum_tensor(f"ps{i}", [128, 512], F32)) for i in range(8)]
        BPS = [Buf(f"ps{i}") for i in range(8)]
        ph1 = A.mark()
        cst_t = A.alloc("cst", [128, 2], F32); b_t = A.alloc("bqkv", [128, 10], F32); wmix_t = A.alloc("wmix", [128, 8], F32)
        eps_t = A.alloc("eps", [128, 1], F32)
        P.dma("sp", cst_t[:], cst[:, :], writes=[Bc])
        P.dma("sp", b_t[:], bqkv[:, :], writes=[Bc])
        P.dma("sp", wmix_t[:], wmix[:, :], writes=[Bc])
        P.op("pool", lambda e: e.memset(eps_t[:], EPS), writes=[Bc])
        Ctab = A.alloc("Ctab", [128, S], F32); Stab = A.alloc("Stab", [128, S], F32)
        BC, BS = Buf("C"), Buf("S")
        W = {}
        BW = {}
        for nm in ("q", "qr", "k", "kr", "v"):
            W[nm] = A.alloc("w" + nm, [128, 8, 256], BF16)
            BW[nm] = Buf("w" + nm)
        rope_tables(P, A, pos[0:1, :], cst_t[:, 0:1], cst_t[:, 1:2], Bc, Ctab, Stab, BC, BS)
        class _St(Stager):
            def __init__(self, P, A, n=3, width=1024):
                self.P = P
                self.tiles = [A.alloc(f"stg{i}", [128, width], F32) for i in range(n)]
                self.bufs = [Buf(f"stg{i}") for i in range(n)]
                self.i = 0; self.ce = 0; self.width = width
        stg = _St(P, A, n=2, width=2048)
        for nm, src in (("q", wq), ("qr", wqr), ("k", wk), ("kr", wkr), ("v", wv)):
            sv = src.rearrange("(k p) n -> p k n", p=128)
            t = stg.tiles[stg.i]; b = stg.bufs[stg.i]; stg.i = (stg.i + 1) % 2
            P.dma("sp", t[:, 0:2048].rearrange("p (k n) -> p k n", k=8), sv, writes=[b])
            dstw = W[nm]
            P.op("pool", lambda e, t=t, dstw=dstw: e.tensor_copy(out=dstw[:].rearrange("p k n -> p (k n)"), in_=t[:, 0:2048]),
                 reads=[b], writes=[BW[nm]])
        xt = [A.alloc(f"xt{i}", [128, 8, 512], F32) for i in range(2)]; Bxt = [Buf(f"xt{i}") for i in range(2)]
        sq = A.alloc("sq", [128, 2, 512], BF16); Bsq = [Buf("sq0"), Buf("sq1")]
        rstd = A.alloc("rstd", [128, 512], F32); Brstd = Buf("rstd")
        hT = A.alloc("hT", [128, 8, 512], BF16); BhT = Buf("hT")
        tA = [A.alloc(f"tA{i}", [128, 512], F32) for i in range(2)]; BtA = [Buf(f"tA{i}") for i in range(2)]
        tB = [A.alloc(f"tB{i}", [128, 512], F32) for i in range(2)]; BtB = [Buf(f"tB{i}") for i in range(2)]

        psr = [0]
        def nextps():
            i = 1 + psr[0] % 7
            psr[0] += 1
            return i
        rot = [0]
        NTI = S // 512
        P.dma("sp", xt[0][:], xT_v[:, :, 0:512], writes=[Bxt[0]])
        for ti in range(NTI):
            t0 = ti * 512
            X, BX = xt[ti % 2], Bxt[ti % 2]
            if ti + 1 < NTI:
                P.dma("sp", xt[(ti + 1) % 2][:], xT_v[:, :, t0 + 512:t0 + 1024], writes=[Bxt[(ti + 1) % 2]])
            rms_tile(P, X, BX, 512, ones_b, Bc, sq, Bsq, PS[0], BPS[0], None, None, rstd, Brstd, eps_t)
            for k in range(8):
                P.op("dve", lambda e, k=k, X=X: e.scalar_tensor_tensor(out=hT[:, k, :], in0=X[:, k, :], scalar=wmix_t[:, k:k + 1],
                                                                      in1=rstd[:, :], op0=ALU.mult, op1=ALU.mult),
                     reads=[BX, Brstd, Bc], writes=[BhT])

            def proj(nm, c):
                pi = nextps()
                for k in range(8):
                    P.op("pe", lambda e, k=k, pi=pi: e.matmul(PS[pi][:, :], W[nm][:, k, c * 128:(c + 1) * 128], hT[:, k, :],
                                                             start=(k == 0), stop=(k == 7)),
                         reads=[BW[nm], BhT], writes=[BPS[pi]])
                return pi
            for c in range(2):
                for nm, nmr, DST, BD, bcol in (("q", "qr", QT, BQ, 0), ("k", "kr", KT, BK, 4)):
                    j = rot[0] % 2
                    rot[0] += 1
                    p1 = proj(nm, c)
                    P.op("dve", lambda e, p1=p1, j=j, c=c, bcol=bcol, t0=t0: e.scalar_tensor_tensor(
                        out=tA[j][:, :], in0=PS[p1][:, :], scalar=b_t[:, bcol + c:bcol + c + 1], in1=Ctab[:, t0:t0 + 512],
                        op0=ALU.add, op1=ALU.mult), reads=[BPS[p1], Bc, BC], writes=[BtA[j]])
                    p2 = proj(nmr, c)
                    P.op("dve", lambda e, p2=p2, j=j, c=c, bcol=bcol, t0=t0: e.scalar_tensor_tensor(
                        out=tB[j][:, :], in0=PS[p2][:, :], scalar=b_t[:, bcol + 2 + c:bcol + 3 + c], in1=Stab[:, t0:t0 + 512],
                        op0=ALU.add, op1=ALU.mult), reads=[BPS[p2], Bc, BS], writes=[BtB[j]])
                    P.op("pool", lambda e, j=j, c=c, DST=DST, t0=t0: e.tensor_tensor(out=DST[:, c, t0:t0 + 512], in0=tA[j][:, :], in1=tB[j][:, :], op=ALU.add),
                         reads=[BtA[j], BtB[j]], writes=[BD[c][ti]])
                p3 = proj("v", c)
                P.op("act", lambda e, p3=p3, c=c, t0=t0: e.activation(out=VT[:, c, t0:t0 + 512], in_=PS[p3][:, :], func=AF.Identity,
                                                              bias=b_t[:, 8 + c:9 + c], scale=1.0),
                     reads=[BPS[p3], Bc], writes=[BV[c][ti]])

        barrier(P)
        A.release(ph1)
        NU = 32
        Vd = {d: A.alloc(f"Vd{d}", [128, NU, 256], BF16) for d in PATTERNS}
        BVd = {d: [Buf(f"Vd{d}_{u}") for u in range(NU)] for d in PATTERNS}
        num = A.alloc("num", [128, 2, S], F32); den = A.alloc("den", [128, 2, S], F32)
        Bnum = [Buf("num0"), Buf("num1")]; Bden = [Buf("den0"), Buf("den1")]
        pexp = [A.alloc(f"pexp{i}", [128, 256], BF16) for i in range(4)]; Bpexp = [Buf(f"pexp{i}") for i in range(4)]
        pmk = [A.alloc(f"pmk{i}", [128, 256], BF16) for i in range(4)]; Bpmk = [Buf(f"pmk{i}") for i in range(4)]
        yout = [A.alloc(f"yout{i}", [128, 1024], BF16) for i in range(2)]; Byout = [Buf(f"yout{i}") for i in range(2)]
        allQ = [b for c in range(2) for b in BQ[c]]
        allK = [b for c in range(2) for b in BK[c]]
        allV = [b for c in range(2) for b in BV[c]]

        def unit_tokens(d, r, j):
            return ss(r + d * 128 * j, 128, d)

        PSB = [p[:].bitcast(BF16) for p in PS]
        tcount = 0
        for d in PATTERNS:
            L = S // d
            nb = L // 128
            for r in range(d):
                for j in range(nb):
                    u = r * nb + j
                    bi = 6 + (tcount % 2)
                    tcount += 1
                    for c in range(2):
                        P.op("pe", lambda e, c=c, d=d, r=r, j=j, bi=bi: e.transpose(PSB[bi][:, c * 128:(c + 1) * 128], VT[:, c, unit_tokens(d, r, j)], ident_t[:]),
                             reads=allV + [Bc], writes=[BPS[bi]])
                    eng = "dve" if tcount % 2 else "act"
                    if eng == "dve":
                        P.op("dve", lambda e, d=d, u=u, bi=bi: e.tensor_copy(out=Vd[d][:, u, :], in_=PSB[bi][:, 0:256]), reads=[BPS[bi]], writes=[BVd[d][u]])
                    else:
                        P.op("act", lambda e, d=d, u=u, bi=bi: e.copy(out=Vd[d][:, u, :], in_=PSB[bi][:, 0:256]), reads=[BPS[bi]], writes=[BVd[d][u]])

        units = []
        for d in PATTERNS:
            nb = (S // d) // 128
            for r in range(d):
                for jq in range(nb):
                    for c in range(2):
                        units.append((d, r, jq, c))
        first_pat = PATTERNS[0]

        def rec_S(ui):
            d, r, jq, c = units[ui]
            nb = (S // d) // 128
            for hh in range(2):
                bi = (ui % 2) * 2 + hh
                rows = slice(hh * 64, hh * 64 + 64)
                qtok = unit_tokens(d, r, jq)
                if jq > 0:
                    P.op("pe", lambda e, bi=bi, rows=rows, c=c, d=d, r=r, jq=jq, qtok=qtok: e.matmul(
                        PS[bi][:, 0:128], KT[rows, c, unit_tokens(d, r, jq - 1)], QT[rows, c, qtok], start=True, stop=True),
                        reads=allK + allQ, writes=[BPS[bi]])
                P.op("pe", lambda e, bi=bi, rows=rows, c=c, d=d, r=r, jq=jq, qtok=qtok: e.matmul(
                    PS[bi][:, 128:256], KT[rows, c, unit_tokens(d, r, jq)], QT[rows, c, qtok], start=True, stop=True),
                    reads=allK + allQ, writes=[BPS[bi]])

        def rec_rest(ui):
            d, r, jq, c = units[ui]
            nb = (S // d) // 128
            lo = 0 if jq > 0 else 128
            po = 4 + ui % 2
            for hh in range(2):
                bi = (ui % 2) * 2 + hh
                pj = (ui % 2) * 2 + hh
                P.op("act", lambda e, bi=bi, pj=pj, lo=lo: e.activation(out=pexp[pj][:, lo:256], in_=PS[bi][:, lo:256], func=AF.Exp, scale=0.125),
                     reads=[BPS[bi]], writes=[Bpexp[pj]])
                meng = "dve" if hh == 0 else "pool"
                P.op(meng, lambda e, pj=pj, lo=lo: e.tensor_tensor(out=pmk[pj][:, lo:256], in0=pexp[pj][:, lo:256], in1=mask_t[:, lo:256], op=ALU.mult),
                     reads=[Bpexp[pj], Bc], writes=[Bpmk[pj]])
            for hh in range(2):
                pj = (ui % 2) * 2 + hh
                prow = slice(hh * 64, hh * 64 + 64)
                tp = (0, hh * 64)
                u_cur = r * nb + jq
                vcol = slice(c * 128 + hh * 64, c * 128 + hh * 64 + 64)
                if jq > 0:
                    P.op("pe", lambda e, po=po, prow=prow, tp=tp, d=d, u=u_cur - 1, vcol=vcol, pj=pj: e.matmul(
                        PS[po][prow, 0:128], Vd[d][:, u, vcol], pmk[pj][:, 0:128], start=True, stop=False, tile_position=tp, skip_group_check=True),
                        reads=[BVd[d][u_cur - 1], Bpmk[pj]], writes=[BPS[po]])
                P.op("pe", lambda e, po=po, prow=prow, tp=tp, d=d, u=u_cur, vcol=vcol, pj=pj, jq=jq: e.matmul(
                    PS[po][prow, 0:128], Vd[d][:, u, vcol], pmk[pj][:, 128:256], start=(jq == 0), stop=True, tile_position=tp, skip_group_check=True),
                    reads=[BVd[d][u_cur], Bpmk[pj]], writes=[BPS[po]])
                if jq > 0:
                    P.op("pe", lambda e, po=po, prow=prow, tp=tp, pj=pj: e.matmul(
                        PS[po][prow, 128:256], ones_b[:, 0:64], pmk[pj][:, 0:128], start=False, stop=False, tile_position=tp, skip_group_check=True),
                        reads=[Bc, Bpmk[pj]], writes=[BPS[po]])
                P.op("pe", lambda e, po=po, prow=prow, tp=tp, pj=pj: e.matmul(
                    PS[po][prow, 128:256], ones_b[:, 0:64], pmk[pj][:, 128:256], start=False, stop=True, tile_position=tp, skip_group_check=True),
                    reads=[Bc, Bpmk[pj]], writes=[BPS[po]])
            qtok = unit_tokens(d, r, jq)
            if d == first_pat:
                P.op("dve", lambda e, po=po, c=c, qtok=qtok: e.tensor_copy(out=num[:, c, qtok], in_=PS[po][:, 0:128]),
                     reads=[BPS[po]], writes=[Bnum[c]])
                P.op("dve", lambda e, po=po, c=c, qtok=qtok: e.tensor_copy(out=den[:, c, qtok], in_=PS[po][:, 128:256]),
                     reads=[BPS[po]], writes=[Bden[c]])
            else:
                P.op("dve", lambda e, po=po, c=c, qtok=qtok: e.tensor_tensor(out=num[:, c, qtok], in0=PS[po][:, 0:128], in1=num[:, c, qtok], op=ALU.add),
                     reads=[BPS[po], Bnum[c]], writes=[Bnum[c]])
                P.op("dve", lambda e, po=po, c=c, qtok=qtok: e.tensor_tensor(out=den[:, c, qtok], in0=PS[po][:, 128:256], in1=den[:, c, qtok], op=ALU.add),
                     reads=[BPS[po], Bden[c]], writes=[Bden[c]])

        rec_S(0)
        for ui in range(len(units)):
            if ui + 1 < len(units):
                rec_S(ui + 1)
            rec_rest(ui)

        oi = 0
        for c in range(2):
            for q4 in range(4):
                cols = slice(q4 * 1024, (q4 + 1) * 1024)
                j = oi % 2
                oi += 1
                P.op("dve", lambda e, c=c, cols=cols: e.reciprocal(out=den[:, c, cols], in_=den[:, c, cols]), reads=[Bden[c]], writes=[Bden[c]])
                P.op("pool", lambda e, c=c, cols=cols, j=j: e.tensor_tensor(out=yout[j][:, :], in0=num[:, c, cols], in1=den[:, c, cols], op=ALU.mult),
                     reads=[Bnum[c], Bden[c]], writes=[Byout[j]])
                P.dma("sp", ya_v[:, c, cols], yout[j][:, :], reads=[Byout[j]])
        P.wait_all("sp", Byout)
        P.emit(st)
    return nc


D = 1024
S = 4096
EPS = 1e-6
NB = 32
HD = 128
LN_S = math.log(HD ** -0.5)


def build_mlstm():
    nc = bass.Bass("TRN2", target_bir_lowering=False)
    dt = lambda name, shape, dtype, kind="ExternalInput": nc.dram_tensor(name, shape, dtype, kind=kind).ap()
    xT = dt("xT", [D, S], F32)
    wmix = dt("wmix", [128, 8], F32)
    wq = dt("wq", [D, 256], F32); wk = dt("wk", [D, 256], F32)
    wvo = dt("wvo", [D, 512], F32)
    wif = dt("wif", [D, 4], F32)
    bqk = dt("bqk", [128, 4], F32)
    cw = dt("cw", [128, 16], F32)
    cb = dt("cb", [128, 4], F32)
    bvo = dt("bvo", [128, 512], F32)
    bif = dt("bif", [128, 4], F32)
    nw = dt("nw", [128, 256], F32)
    sc = dt("sc", [128, 576], F32)
    cmask = dt("cmask", [128, 128], F32)
    identb = dt("identb", [128, 128], BF16)
    ybT = dt("ybT", [256, S], BF16, kind="ExternalOutput")

    xT_v = xT.rearrange("(k p) t -> p k t", p=128)
    yb_v = ybT.rearrange("(c p) t -> p c t", p=128)

    with ExitStack() as st:
        P = Prog(nc)
        A = Arena(nc)
        PS = [st.enter_context(nc.psum_tensor(f"ps{i}", [128, 512], F32)) for i in range(8)]
        BPS = [Buf(f"ps{i}") for i in range(8)]
        PSB = [p[:].bitcast(BF16) for p in PS]
        Bc = Buf("consts")
        QT = A.alloc("QT", [128, 2, S], BF16); KT = A.alloc("KT", [128, 2, S], BF16)
        BQ = [Buf(f"Q{t}") for t in range(8)]; BK = [Buf(f"K{t}") for t in range(8)]
        Vaug = A.alloc("Vaug", [128, NB, 2, 130], BF16); BVa = [Buf(f"V{b}") for b in range(NB)]
        G = A.alloc("G", [128, NB, 256], BF16); BG = [Buf(f"G{b}") for b in range(NB)]
        YT = A.alloc("YT", [128, 2, S], BF16); BYT = [Buf(f"YT{i}") for i in range(8)]
        GT = A.alloc("GT", [128, 4, NB], F32); BGT = Buf("GT")
        sc_t = A.alloc("sc", [128, 576], F32)
        cmask_t = A.alloc("cmask", [128, 128], F32)
        identb_t = A.alloc("identb", [128, 128], BF16)
        ones_b = A.alloc("ones_b", [128, 128], BF16)
        eps_t = A.alloc("eps", [128, 1], F32); one_t = A.alloc("one", [128, 1], F32); lns_t = A.alloc("lns", [128, 1], F32)
        tri = sc_t[:, 0:128]; onesf = sc_t[:, 128:256]; identf = sc_t[:, 256:384]
        Mexcl = sc_t[0:64, 384:448]; Mneg = sc_t[0:64, 448:512]; Shift = sc_t[0:64, 512:576]
        P.dma("sp", sc_t[:], sc[:, :], writes=[Bc])
        P.dma("sp", cmask_t[:], cmask[:, :], writes=[Bc])
        P.dma("sp", identb_t[:], identb[:, :], writes=[Bc])
        P.op("pool", lambda e: e.memset(ones_b[:], 1.0), writes=[Bc])
        P.op("pool", lambda e: e.memset(eps_t[:], EPS), writes=[Bc])
        P.op("pool", lambda e: e.memset(one_t[:], 1.0), writes=[Bc])
        P.op("pool", lambda e: e.memset(lns_t[:], LN_S), writes=[Bc])
        P.op("pool", lambda e: e.memset(Vaug[:, :, :, 128:130], 1.0), writes=BVa)
        wmix_t = A.alloc("wmix", [128, 8], F32); bqk_t = A.alloc("bqk", [128, 4], F32); cw_t = A.alloc("cw", [128, 16], F32)
        cb_t = A.alloc("cb", [128, 4], F32); bvo_t = A.alloc("bvo", [128, 512], F32); bif_t = A.alloc("bif", [128, 4], F32)
        nw_t = A.alloc("nw", [128, 256], F32)
        for t_, s_ in ((wmix_t, wmix), (bqk_t, bqk), (cw_t, cw), (cb_t, cb), (bvo_t, bvo), (bif_t, bif), (nw_t, nw)):
            P.dma("sp", t_[:], s_[:, :], writes=[Bc])
        Wq = A.alloc("Wq", [128, 8, 256], BF16); Wk = A.alloc("Wk", [128, 8, 256], BF16)
        Wvo = A.alloc("Wvo", [128, 8, 512], BF16); Wif = A.alloc("Wif", [128, 8, 4], BF16)
        BWq, BWk, BWvo, BWif = Buf("Wq"), Buf("Wk"), Buf("Wvo"), Buf("Wif")
        stg = [A.alloc("stg0", [128, 8, 512], F32)] * 2; Bstg = [Buf("stg0")] * 2
        si = 0
        for src, dstw, bw, n in ((wq, Wq, BWq, 256), (wk, Wk, BWk, 256), (wvo, Wvo, BWvo, 512), (wif, Wif, BWif, 4)):
            sv = src.rearrange("(k p) n -> p k n", p=128)
            t = stg[si % 2]; b = Bstg[si % 2]; si += 1
            P.dma("sp", t[:, :, 0:n], sv, writes=[b])
            P.op("pool", lambda e, t=t, dstw=dstw, n=n: e.tensor_copy(out=dstw[:], in_=t[:, :, 0:n]), reads=[b], writes=[bw])
        xt = [A.alloc(f"xt{i}", [128, 8, 512], F32) for i in range(2)]; Bxt = [Buf(f"xt{i}") for i in range(2)]
        sq = A.alloc("sq", [128, 2, 512], BF16); Bsq = [Buf("sq0"), Buf("sq1")]
        rstd = A.alloc("rstd", [128, 512], F32); Brstd = Buf("rstd")
        hT = A.alloc("hT", [128, 8, 512], BF16); BhT = Buf("hT")
        Z = [A.alloc(f"Z{i}", [128, 515], F32) for i in range(4)]; BZ = [Buf(f"Z{i}") for i in range(4)]
        acc = [A.alloc(f"acc{i}", [128, 512], F32) for i in range(2)]; Bacc = [Buf(f"acc{i}") for i in range(2)]
        ot = [A.alloc(f"ot{i}", [128, 256], F32) for i in range(2)]; Bot = [Buf(f"ot{i}") for i in range(2)]
        ptmp = A.alloc("ptmp", [128, 512], F32); Bptmp = Buf("ptmp")
        for i in range(4):
            P.op("pool", lambda e, i=i: e.memset(Z[i][:, 0:3], 0.0), writes=[BZ[i]])

        psr = [0]
        def nextps():
            i = 1 + psr[0] % 7
            psr[0] += 1
            return i
        NTI = S // 512
        P.dma("sp", xt[0][:], xT_v[:, :, 0:512], writes=[Bxt[0]])
        cnt = 0
        for ti in range(NTI):
            t0 = ti * 512
            X, BX = xt[ti % 2], Bxt[ti % 2]
            if ti + 1 < NTI:
                P.dma("sp", xt[(ti + 1) % 2][:], xT_v[:, :, t0 + 512:t0 + 1024], writes=[Bxt[(ti + 1) % 2]])
            rms_tile(P, X, BX, 512, ones_b, Bc, sq, Bsq, PS[0], BPS[0], None, None, rstd, Brstd, eps_t)
            for k in range(8):
                P.op("dve", lambda e, k=k, X=X: e.scalar_tensor_tensor(out=hT[:, k, :], in0=X[:, k, :], scalar=wmix_t[:, k:k + 1],
                                                                      in1=rstd[:, :], op0=ALU.mult, op1=ALU.mult),
                     reads=[BX, Brstd, Bc], writes=[BhT])
            for si_, (Wm, BWm, DST, BD) in enumerate(((Wq, BWq, QT, BQ), (Wk, BWk, KT, BK))):
                for hd in range(2):
                    zi = si_ * 2 + hd
                    z, Bz = Z[zi], BZ[zi]
                    pi = nextps()
                    for k in range(8):
                        P.op("pe", lambda e, k=k, pi=pi, Wm=Wm, hd=hd: e.matmul(PS[pi][:, :], Wm[:, k, hd * 128:(hd + 1) * 128], hT[:, k, :],
                                                                               start=(k == 0), stop=(k == 7)),
                             reads=[BWm, BhT], writes=[BPS[pi]])
                    if ti > 0:
                        P.op("act", lambda e, z=z: e.copy(out=z[:, 0:3], in_=z[:, 512:515]), reads=[Bz], writes=[Bz])
                    P.op("act", lambda e, z=z, pi=pi, zi=zi: e.activation(out=z[:, 3:515], in_=PS[pi][:, :], func=AF.Identity,
                                                                         bias=bqk_t[:, zi:zi + 1], scale=1.0),
                         reads=[BPS[pi], Bc], writes=[Bz])
                    ce = "dve" if zi % 2 == 0 else "pool"
                    a, Ba = acc[zi % 2], Bacc[zi % 2]
                    P.op(ce, lambda e, z=z, a=a, zi=zi: e.tensor_scalar(out=a[:, :], in0=z[:, 0:512], scalar1=cw_t[:, zi * 4:zi * 4 + 1], scalar2=None, op0=ALU.mult),
                         reads=[Bz, Bc], writes=[Ba])
                    for tap in range(1, 4):
                        if ce == "dve":
                            P.op(ce, lambda e, z=z, a=a, zi=zi, tap=tap: e.scalar_tensor_tensor(
                                out=a[:, :], in0=z[:, tap:tap + 512], scalar=cw_t[:, zi * 4 + tap:zi * 4 + tap + 1], in1=a[:, :], op0=ALU.mult, op1=ALU.add),
                                reads=[Bz, Bc, Ba], writes=[Ba])
                        else:
                            P.op(ce, lambda e, z=z, zi=zi, tap=tap: e.tensor_scalar(out=ptmp[:, :], in0=z[:, tap:tap + 512],
                                                                                  scalar1=cw_t[:, zi * 4 + tap:zi * 4 + tap + 1], scalar2=None, op0=ALU.mult),
                                 reads=[Bz, Bc], writes=[Bptmp])
                            P.op(ce, lambda e, a=a: e.tensor_tensor(out=a[:, :], in0=a[:, :], in1=ptmp[:, :], op=ALU.add),
                                 reads=[Bptmp, Ba], writes=[Ba])
                    P.op("act", lambda e, a=a, zi=zi, DST=DST, hd=hd, t0=t0: e.activation(out=DST[:, hd, t0:t0 + 512], in_=a[:, :], func=AF.Silu,
                                                                                         bias=cb_t[:, zi:zi + 1], scale=1.0),
                         reads=[Ba, Bc], writes=[BD[ti]])
            for bb in range(4):
                blk = ti * 4 + bb
                tsl = slice(bb * 128, (bb + 1) * 128)
                pv = nextps()
                for k in range(8):
                    P.op("pe", lambda e, k=k, pv=pv, tsl=tsl: e.matmul(PS[pv][:, :], hT[:, k, tsl], Wvo[:, k, :], start=(k == 0), stop=(k == 7)),
                         reads=[BWvo, BhT], writes=[BPS[pv]])
                pg = nextps()
                for k in range(8):
                    P.op("pe", lambda e, k=k, pg=pg, tsl=tsl: e.matmul(PS[pg][:, 0:4], hT[:, k, tsl], Wif[:, k, :], start=(k == 0), stop=(k == 7)),
                         reads=[BWif, BhT], writes=[BPS[pg]])
                P.op("dve", lambda e, pv=pv, blk=blk: e.tensor_tensor(out=Vaug[:, blk, :, 0:128], in0=PS[pv][:, 0:256].rearrange("p (h d) -> p h d", h=2),
                                                                   in1=bvo_t[:, 0:256].rearrange("p (h d) -> p h d", h=2), op=ALU.add),
                     reads=[BPS[pv], Bc], writes=[BVa[blk]])
                j = cnt % 2
                cnt += 1
                P.op("dve", lambda e, pv=pv, j=j: e.tensor_tensor(out=ot[j][:, :], in0=PS[pv][:, 256:512], in1=bvo_t[:, 256:512], op=ALU.add),
                     reads=[BPS[pv], Bc], writes=[Bot[j]])
                P.op("act", lambda e, j=j: e.activation(out=ot[j][:, :], in_=ot[j][:, :], func=AF.Sigmoid), reads=[Bot[j]], writes=[Bot[j]])
                P.op("pool", lambda e, j=j, blk=blk: e.tensor_tensor(out=G[:, blk, :], in0=ot[j][:, :], in1=nw_t[:, :], op=ALU.mult),
                     reads=[Bot[j], Bc], writes=[BG[blk]])
                P.op("dve", lambda e, pg=pg, blk=blk: e.tensor_tensor(out=GT[:, :, blk], in0=PS[pg][:, 0:4], in1=bif_t[:, :], op=ALU.add),
                     reads=[BPS[pg], Bc], writes=[BGT])

        IG = GT[:, 0:2, :].rearrange("p h b -> p (h b)")
        FG = GT[:, 2:4, :].rearrange("p h b -> p (h b)")
        sA = A.alloc("sA", [128, 64], F32); sB = A.alloc("sB", [128, 64], F32); U = A.alloc("U", [128, 64], F32)
        Bsb = A.alloc("Bsb", [128, 64], F32); TOT = A.alloc("TOT", [64, 128], F32); Dg = A.alloc("Dg", [64, 128], F32)
        Y = A.alloc("Y", [64, 64], F32); Rcol = A.alloc("Rcol", [64, 2], F32); Rbc = A.alloc("Rbc", [128, 64], F32)
        WGT = A.alloc("WGT", [128, 64], F32); FLOOR = A.alloc("FLOOR", [128, 64], F32); ALPHA = A.alloc("ALPHA", [128, 64], F32)
        Bs = Buf("scan")
        p7, B7 = PS[7], BPS[7]
        p6, B6 = PS[6], BPS[6]
        P.op("act", lambda e: e.activation(out=sA[:, :], in_=FG, func=AF.Exp, scale=-1.0), reads=[BGT], writes=[Bs])
        P.op("act", lambda e: e.activation(out=sA[:, :], in_=sA[:, :], func=AF.Ln, bias=one_t[:, 0:1], scale=1.0), reads=[Bs, Bc], writes=[Bs])
        P.op("dve", lambda e: e.tensor_scalar(out=sA[:, :], in0=sA[:, :], scalar1=-1.0, scalar2=None, op0=ALU.mult), reads=[Bs], writes=[Bs])
        P.op("pe", lambda e: e.matmul(p7[0:64, 0:128], sA[:, :], onesf, start=True, stop=True), reads=[Bs, Bc], writes=[B7])
        P.op("dve", lambda e: e.tensor_copy(out=TOT[:, :], in_=p7[0:64, 0:128]), reads=[B7], writes=[Bs])
        P.op("pe", lambda e: e.matmul(p6[:, 0:64], tri, sA[:, :], start=True, stop=False), reads=[Bs, Bc], writes=[B6])
        P.op("pe", lambda e: e.matmul(p6[:, 0:64], TOT[:, :], Mexcl, start=False, stop=True), reads=[Bs, Bc], writes=[B6])
        P.op("dve", lambda e: e.tensor_copy(out=Bsb[:, :], in_=p6[:, 0:64]), reads=[B6], writes=[Bs])
        P.op("dve", lambda e: e.tensor_tensor(out=U[:, :], in0=IG, in1=Bsb[:, :], op=ALU.subtract), reads=[Bs, BGT], writes=[Bs])
        P.op("pe", lambda e: e.transpose(p7[0:64, 0:128], U[:, :], identf), reads=[Bs, Bc], writes=[B7])
        P.op("dve", lambda e: e.reduce_max(out=Rcol[:, 0:1], in_=p7[0:64, 0:128], axis=AX.X), reads=[B7], writes=[Bs])
        P.op("dve", lambda e: e.tensor_scalar(out=Dg[:, 0:64], in0=identf[0:64, 0:64], scalar1=Rcol[:, 0:1], scalar2=None, op0=ALU.mult), reads=[Bs, Bc], writes=[Bs])
        P.op("pe", lambda e: e.matmul(p6[0:64, 0:64], onesf[0:64, 0:64], Dg[:, 0:64], start=True, stop=True), reads=[Bs, Bc], writes=[B6])
        P.op("dve", lambda e: e.tensor_tensor(out=Y[:, :], in0=p6[0:64, 0:64], in1=Mneg, op=ALU.add), reads=[B6, Bc], writes=[Bs])
        P.op("dve", lambda e: e.reduce_max(out=Rcol[:, 1:2], in_=Y[:, :], axis=AX.X), reads=[Bs], writes=[Bs])
        P.op("dve", lambda e: e.tensor_scalar(out=Rcol[:, 1:2], in0=Rcol[:, 1:2], scalar1=0.0, scalar2=None, op0=ALU.max), reads=[Bs], writes=[Bs])
        P.op("dve", lambda e: e.tensor_scalar(out=Dg[:, 0:64], in0=identf[0:64, 0:64], scalar1=Rcol[:, 1:2], scalar2=None, op0=ALU.mult), reads=[Bs, Bc], writes=[Bs])
        P.op("dve", lambda e: e.tensor_scalar(out=Dg[:, 64:128], in0=Shift, scalar1=Rcol[:, 1:2], scalar2=None, op0=ALU.mult), reads=[Bs, Bc], writes=[Bs])
        P.op("pe", lambda e: e.matmul(p7[:, 0:128], onesf[0:64, :], Dg[:, :], start=True, stop=True), reads=[Bs, Bc], writes=[B7])
        P.op("dve", lambda e: e.tensor_copy(out=Rbc[:, :], in_=p7[:, 0:64]), reads=[B7], writes=[Bs])
        P.op("dve", lambda e: e.tensor_tensor(out=sB[:, :], in0=p7[:, 64:128], in1=Rbc[:, :], op=ALU.subtract), reads=[B7, Bs], writes=[Bs])
        P.op("act", lambda e: e.activation(out=ALPHA[:, :], in_=sB[:, :], func=AF.Exp), reads=[Bs], writes=[Bs])
        P.op("dve", lambda e: e.tensor_tensor(out=sB[:, :], in0=U[:, :], in1=Rbc[:, :], op=ALU.subtract), reads=[Bs], writes=[Bs])
        P.op("act", lambda e: e.activation(out=WGT[:, :], in_=sB[:, :], func=AF.Exp, bias=lns_t[:, 0:1], scale=1.0), reads=[Bs, Bc], writes=[Bs])
        P.op("dve", lambda e: e.tensor_tensor(out=sB[:, :], in0=Bsb[:, :], in1=Rbc[:, :], op=ALU.add), reads=[Bs], writes=[Bs])
        P.op("act", lambda e: e.activation(out=FLOOR[:, :], in_=sB[:, :], func=AF.Exp, scale=-1.0), reads=[Bs], writes=[Bs])

        Kp = [A.alloc(f"Kp{i}", [128, 2, 128], BF16) for i in range(2)]; BKp = [Buf(f"Kp{i}") for i in range(2)]
        Pt = [A.alloc(f"Pt{i}", [128, 2, 128], BF16) for i in range(2)]; BPt = [Buf(f"Pt{i}") for i in range(2)]
        T = A.alloc("T", [128, 2, 130], F32); BT = [Buf("T0"), Buf("T1")]
        Ct = [A.alloc(f"Ct{i}", [128, 2, 130], BF16) for i in range(2)]; BCt = [[Buf(f"Ct{i}_{h}") for h in range(2)] for i in range(2)]
        nbuf = [A.alloc(f"nb{i}", [128, 2, 130], F32) for i in range(2)]; Bnb = [Buf(f"nb{i}") for i in range(2)]
        junk = A.alloc("junk", [128, 128], BF16); Bjunk = Buf("junk")
        sm = [A.alloc(f"sm{i}", [128, 8], F32) for i in range(2)]; Bsm = [Buf(f"sm{i}") for i in range(2)]
        yb = [A.alloc(f"yb{i}", [128, 256], BF16) for i in range(2)]; Byb = [Buf(f"yb{i}") for i in range(2)]
        P.op("pool", lambda e: e.memset(T[:], 0.0), writes=BT)
        for b in range(NB):
            i2 = b % 2
            tok = slice(b * 128, (b + 1) * 128)
            tq = b // 4
            for hd in range(2):
                P.op("pe", lambda e, hd=hd, tok=tok: e.transpose(PSB[0][:, hd * 128:(hd + 1) * 128], KT[:, hd, tok], identb_t[:]),
                     reads=[BK[tq], Bc], writes=[BPS[0]])
            for hd in range(2):
                col = hd * 32 + b
                P.op("act", lambda e, hd=hd, col=col, i2=i2: e.activation(out=Kp[i2][:, hd, :], in_=PSB[0][:, hd * 128:(hd + 1) * 128], func=AF.Copy,
                                                                         scale=WGT[:, col:col + 1]),
                     reads=[BPS[0], Bs], writes=[BKp[i2]])
            for hd in range(2):
                P.op("pe", lambda e, hd=hd, tok=tok: e.matmul(PS[1][:, hd * 128:(hd + 1) * 128], KT[:, hd, tok], QT[:, hd, tok], start=True, stop=True),
                     reads=[BK[tq], BQ[tq]], writes=[BPS[1]])
            for hd in range(2):
                col = hd * 32 + b
                P.op("dve", lambda e, hd=hd, col=col, i2=i2: e.scalar_tensor_tensor(out=Pt[i2][:, hd, :], in0=PS[1][:, hd * 128:(hd + 1) * 128],
                                                                                   scalar=WGT[:, col:col + 1], in1=cmask_t[:, :], op0=ALU.mult, op1=ALU.mult),
                     reads=[BPS[1], Bs, Bc], writes=[BPt[i2]])
            if b > 0:
                for hd in range(2):
                    col = hd * 32 + b
                    P.op("act", lambda e, hd=hd, col=col, i2=i2: e.activation(out=Ct[i2][:, hd, 0:129], in_=T[:, hd, 0:129], func=AF.Copy,
                                                                             scale=ALPHA[:, col:col + 1]),
                         reads=[BT[hd], Bs], writes=[BCt[i2][hd]])
            pn = 2 + i2
            for hd in range(2):
                P.op("pe", lambda e, hd=hd, pn=pn, i2=i2, b=b: e.matmul(PS[pn][:, hd * 130:hd * 130 + 129], Pt[i2][:, hd, :], Vaug[:, b, hd, 0:129],
                                                                        start=True, stop=(b == 0), skip_group_check=True),
                     reads=[BPt[i2], BVa[b]], writes=[BPS[pn]])
                if b > 0:
                    P.op("pe", lambda e, hd=hd, pn=pn, i2=i2, tok=tok: e.matmul(PS[pn][:, hd * 130:hd * 130 + 129], QT[:, hd, tok], Ct[i2][:, hd, 0:129],
                                                                              start=False, stop=True, skip_group_check=True),
                         reads=[BQ[tq], BCt[i2][hd]], writes=[BPS[pn]])
            pc = 4 + i2
            if b < NB - 1:
                for hd in range(2):
                    P.op("pe", lambda e, hd=hd, pc=pc, i2=i2, b=b: e.matmul(PS[pc][:, hd * 130:hd * 130 + 129], Kp[i2][:, hd, :], Vaug[:, b, hd, 0:129],
                                                                            start=True, stop=True),
                         reads=[BKp[i2], BVa[b]], writes=[BPS[pc]])
                for hd in range(2):
                    col = hd * 32 + b
                    P.op("dve", lambda e, hd=hd, col=col, pc=pc: e.scalar_tensor_tensor(out=T[:, hd, 0:129], in0=T[:, hd, 0:129], scalar=ALPHA[:, col:col + 1],
                                                                                      in1=PS[pc][:, hd * 130:hd * 130 + 129], op0=ALU.mult, op1=ALU.add),
                         reads=[BT[hd], BPS[pc], Bs], writes=[BT[hd]])
            nbt, Bn = nbuf[i2], Bnb[i2]
            s_, Bs_ = sm[i2], Bsm[i2]
            P.op("dve", lambda e, pn=pn, nbt=nbt: e.tensor_copy(out=nbt[:, :, 0:129], in_=PS[pn][:, 0:260].rearrange("p (h d) -> p h d", h=2)[:, :, 0:129]),
                 reads=[BPS[pn]], writes=[Bn])
            fl = FLOOR[:, b:b + 33:32]
            P.op("dve", lambda e, nbt=nbt, s_=s_: e.scalar_tensor_tensor(out=s_[:, 0:2], in0=nbt[:, :, 128], scalar=-1.0, in1=nbt[:, :, 128],
                                                                        op0=ALU.mult, op1=ALU.max), reads=[Bn], writes=[Bs_])
            P.op("dve", lambda e, s_=s_, fl=fl: e.tensor_tensor(out=s_[:, 0:2], in0=s_[:, 0:2], in1=fl, op=ALU.max),
                 reads=[Bs_, Bs], writes=[Bs_])
            P.op("dve", lambda e, s_=s_: e.reciprocal(out=s_[:, 0:2], in_=s_[:, 0:2]), reads=[Bs_], writes=[Bs_])
            P.op("pool", lambda e, s_=s_: e.memset(s_[:, 2:4], 0.0), writes=[Bs_])
            for hd in range(2):
                P.op("act", lambda e, hd=hd, nbt=nbt, s_=s_: e.activation(out=junk[:, :], in_=nbt[:, hd, 0:128], func=AF.Square,
                                                                         accum_out=s_[:, 2 + hd:3 + hd]),
                     reads=[Bn], writes=[Bjunk, Bs_])
            P.op("dve", lambda e, s_=s_: e.tensor_tensor(out=s_[:, 4:6], in0=s_[:, 2:4], in1=s_[:, 0:2], op=ALU.mult), reads=[Bs_], writes=[Bs_])
            P.op("dve", lambda e, s_=s_: e.tensor_tensor(out=s_[:, 4:6], in0=s_[:, 4:6], in1=s_[:, 0:2], op=ALU.mult), reads=[Bs_], writes=[Bs_])
            P.op("act", lambda e, s_=s_: e.activation(out=s_[:, 4:6], in_=s_[:, 4:6], func=AF.Sqrt, bias=eps_t[:, 0:1], scale=1.0 / HD),
                 reads=[Bs_, Bc], writes=[Bs_])
            P.op("dve", lambda e, s_=s_: e.reciprocal(out=s_[:, 4:6], in_=s_[:, 4:6]), reads=[Bs_], writes=[Bs_])
            P.op("dve", lambda e, s_=s_: e.tensor_tensor(out=s_[:, 6:8], in0=s_[:, 4:6], in1=s_[:, 0:2], op=ALU.mult), reads=[Bs_], writes=[Bs_])
            ybt, Bybt = yb[i2], Byb[i2]
            for hd in range(2):
                P.op("dve", lambda e, hd=hd, nbt=nbt, s_=s_, ybt=ybt, b=b: e.scalar_tensor_tensor(
                    out=ybt[:, hd * 128:(hd + 1) * 128], in0=nbt[:, hd, 0:128], scalar=s_[:, 6 + hd:7 + hd], in1=G[:, b, hd * 128:(hd + 1) * 128],
                    op0=ALU.mult, op1=ALU.mult), reads=[Bn, Bs_, BG[b]], writes=[Bybt])
            for hd in range(2):
                P.op("pe", lambda e, hd=hd, ybt=ybt: e.transpose(PSB[6][:, hd * 128:(hd + 1) * 128], ybt[:, hd * 128:(hd + 1) * 128], identb_t[:]),
                     reads=[Bybt, Bc], writes=[BPS[6]])
            P.op("act", lambda e, tok=tok: e.copy(out=YT[:, :, tok], in_=PSB[6][:, 0:256].rearrange("p (h d) -> p h d", h=2)),
                 reads=[BPS[6]], writes=[BYT[tq]])
            if b % 4 == 3:
                P.dma("sp", yb_v[:, :, tq * 512:(tq + 1) * 512], YT[:, :, tq * 512:(tq + 1) * 512], reads=[BYT[tq]])
        P.wait_all("sp", BYT)
        P.emit(st)
    return nc


import ml_dtypes
from concourse.bass_utils import run_bass_kernel_spmd

_bf = ml_dtypes.bfloat16
COL_AQ, COL_AK, COL_AV = 0, 512, 1024
COL_MQ, COL_MK, COL_MV, COL_MO = 1536, 2048, 2560, 3072
COL_MI, COL_MF, COL_GA = 3584, 3588, 3592
NCORES = 8


def _vec8(v):
    return np.ascontiguousarray(v.reshape(8, 128).T)


def _rep(v):
    return np.ascontiguousarray(np.broadcast_to(v[None, :], (128, v.shape[0]))).astype(np.float32)


def _attn_consts():
    p = np.arange(128)
    rope = (p % 64) < 16
    invf = np.where(rope, 500000.0 ** (-((p % 8) * 2) / 16.0), 0.0).astype(np.float32)
    sgn = np.where((p % 64) < 8, -1.0, 1.0).astype(np.float32)
    kc = np.arange(128)[:, None]
    c = np.arange(256)[None, :]
    mask = np.where(c < 128, kc >= c, kc <= c - 128).astype(np.float32).astype(_bf)
    idx = np.arange(256)
    d = idx % 64
    perm = np.where(d < 8, idx + 8, np.where(d < 16, idx - 8, idx))
    return np.stack([invf, sgn], 1), mask, np.eye(128, dtype=np.float32).astype(_bf), perm


def _scan_consts():
    p = np.arange(128)
    tri = (p[:, None] <= p[None, :]).astype(np.float32)
    ones = np.ones((128, 128), np.float32)
    ident = np.eye(128, dtype=np.float32)
    c = np.arange(64)
    hd = c // 32
    bl = c % 32
    same = hd[:, None] == hd[None, :]
    Mexcl = (same & (bl[:, None] < bl[None, :])).astype(np.float32)
    Mneg = np.where(same & (bl[None, :] <= bl[:, None]), 0.0, -1e30).astype(np.float32)
    Shift = (same & (bl[:, None] == bl[None, :] - 1)).astype(np.float32)
    pad = lambda m: np.concatenate([m, np.zeros((64, 64), np.float32)], 0)
    return np.concatenate([tri, ones, ident, pad(Mexcl), pad(Mneg), pad(Shift)], 1)


def _attn_maps(xT_b, positions, w_in, b_in, wmix):
    cst, mask, ident, perm = _attn_consts()
    maps = []
    for c in range(NCORES):
        b, hh = c // 2, c % 2
        cq = COL_AQ + hh * 256 + np.arange(256)
        ck = COL_AK + hh * 256 + np.arange(256)
        cv = COL_AV + hh * 256 + np.arange(256)
        b2 = lambda cols: b_in[cols].reshape(2, 128).T
        bq = np.concatenate([b2(cq), b2(cq[perm]), b2(ck), b2(ck[perm]), b2(cv)], 1)
        maps.append(dict(
            xT=xT_b[b], pos=np.ascontiguousarray(positions[b:b + 1]).astype(np.int32), wmix=_vec8(wmix),
            wq=np.ascontiguousarray(w_in[:, cq]), wqr=np.ascontiguousarray(w_in[:, cq[perm]]),
            wk=np.ascontiguousarray(w_in[:, ck]), wkr=np.ascontiguousarray(w_in[:, ck[perm]]),
            wv=np.ascontiguousarray(w_in[:, cv]), bqkv=np.ascontiguousarray(bq.astype(np.float32)),
            cst=cst, mask=mask, ident=ident))
    return maps


def _mlstm_maps(xT_b, w_in, b_in, wmix, conv_w, conv_b, mnw):
    sc = _scan_consts()
    cmask = (np.arange(128)[:, None] <= np.arange(128)[None, :]).astype(np.float32)
    identb = np.eye(128, dtype=np.float32).astype(_bf)
    maps = []
    for c in range(NCORES):
        b, hh = c // 2, c % 2
        cq = COL_MQ + hh * 256 + np.arange(256)
        ck = COL_MK + hh * 256 + np.arange(256)
        cv = COL_MV + hh * 256 + np.arange(256)
        co = COL_MO + hh * 256 + np.arange(256)
        cif = np.array([COL_MI + 2 * hh, COL_MI + 2 * hh + 1, COL_MF + 2 * hh, COL_MF + 2 * hh + 1])
        qc = hh * 256 + np.arange(256)
        kc = 512 + hh * 256 + np.arange(256)
        cw = np.concatenate([conv_w[:, qc[:128]].T, conv_w[:, qc[128:]].T, conv_w[:, kc[:128]].T, conv_w[:, kc[128:]].T], 1)
        cb = np.stack([conv_b[qc[:128]], conv_b[qc[128:]], conv_b[kc[:128]], conv_b[kc[128:]]], 1)
        bqk = np.stack([b_in[cq[:128]], b_in[cq[128:]], b_in[ck[:128]], b_in[ck[128:]]], 1)
        maps.append(dict(
            xT=xT_b[b], wmix=_vec8(wmix),
            wq=np.ascontiguousarray(w_in[:, cq]), wk=np.ascontiguousarray(w_in[:, ck]),
            wvo=np.ascontiguousarray(np.concatenate([w_in[:, cv], w_in[:, co]], 1)), wif=np.ascontiguousarray(w_in[:, cif]),
            bqk=np.ascontiguousarray(bqk), cw=np.ascontiguousarray(cw), cb=np.ascontiguousarray(cb),
            bvo=_rep(np.concatenate([b_in[cv], b_in[co]])), bif=_rep(b_in[cif]), nw=_rep(mnw[hh * 256:(hh + 1) * 256]),
            sc=sc, cmask=cmask, identb=identb))
    return maps


def _dense_maps(xT_b, ya, yb, w_in, b_in, wmix, wmlp, wfin, pa, pb, wout, w1, w2):
    maps = []
    wg = np.ascontiguousarray(w_in[:, COL_GA:])
    bg = np.ascontiguousarray(b_in[COL_GA:].reshape(16, 128).T)
    for c in range(NCORES):
        b, th = c // 2, c % 2
        sl = slice(th * 2048, (th + 1) * 2048)
        maps.append(dict(
            xT=np.ascontiguousarray(xT_b[b][:, sl]),
            yaT=np.ascontiguousarray(np.concatenate([ya[2 * b][:, sl], ya[2 * b + 1][:, sl]], 0)),
            ybT=np.ascontiguousarray(np.concatenate([yb[2 * b][:, sl], yb[2 * b + 1][:, sl]], 0)),
            wg=wg, bg=bg, wmix=_vec8(wmix), wmlp=_vec8(wmlp), wfin=_vec8(wfin),
            pa=pa, pb=pb, wout=wout, w1=w1, w2=w2))
    return maps


def kernel(x, positions, norm_mix_w, w_in, b_in, conv_w, conv_b, mlstm_norm_w, w_proj_att, w_proj_mlstm,
           w_out, norm_mlp_w, w_ff1, w_ff2, final_norm_w):
    f = lambda a: np.asarray(a, dtype=np.float32)
    x = f(x)
    positions = np.asarray(positions)
    depth = w_in.shape[0]
    cores = list(range(NCORES))
    xT_b = [np.ascontiguousarray(x[b].T) for b in range(4)]
    for l in range(depth):
        wi, bi = f(w_in[l]), f(b_in[l])
        ra = run_bass_kernel_spmd(build_attn(), _attn_maps(xT_b, positions, wi, bi, f(norm_mix_w[l])), core_ids=cores)
        ya = [r["yaT"] for r in ra.results]
        rm = run_bass_kernel_spmd(build_mlstm(), _mlstm_maps(xT_b, wi, bi, f(norm_mix_w[l]), f(conv_w[l]), f(conv_b[l]), f(mlstm_norm_w[l])),
                                  core_ids=cores)
        yb = [r["ybT"] for r in rm.results]
        rd = run_bass_kernel_spmd(build_dense(l == depth - 1),
                                  _dense_maps(xT_b, ya, yb, wi, bi, f(norm_mix_w[l]), f(norm_mlp_w[l]), f(final_norm_w),
                                              f(w_proj_att[l]), f(w_proj_mlstm[l]), f(w_out[l]), f(w_ff1[l]), f(w_ff2[l])),
                                  core_ids=cores)
        xT_b = [np.concatenate([rd.results[2 * b]["xoT"], rd.results[2 * b + 1]["xoT"]], 1) for b in range(4)]
    return np.ascontiguousarray(np.stack([t.T for t in xT_b], 0)).astype(np.float32)
```
